# Optimizing a Trainium2 kernel written in Bass

```python
import jax
import jax.numpy as jnp
from jax import lax
import numpy as np


D_MODEL = 2048
BATCH = 1
SEQ = 8192
DEPTH = 4

ML_HEADS = 4
ML_DIM = D_MODEL // 16
ML_WIDTH = ML_HEADS * ML_DIM
ML_CONV = 3
FORGET_BIAS_LO = 3.0
FORGET_BIAS_HI = 6.0
NEG_INIT = -1e30
RET_HEADS = 4
RET_QK = D_MODEL // 32
RET_V = D_MODEL // 16
RET_WIDTH = RET_HEADS * RET_V
RET_DECAY_BASE = 5.0
MLA_HEADS = 8
MLA_NOPE = D_MODEL // 16
ROPE_DIM = D_MODEL // 32
MLA_V = D_MODEL // 16
MLA_Q_RANK = D_MODEL // 4
MLA_KV_RANK = D_MODEL // 8
MLA_WIDTH = MLA_HEADS * MLA_V
MIX_WIDTH = ML_WIDTH + RET_WIDTH + MLA_WIDTH
IN_SIZES = (ML_WIDTH, ML_WIDTH, ML_WIDTH, ML_WIDTH, 4 * ML_HEADS,
            RET_HEADS * RET_QK, RET_HEADS * RET_QK, RET_WIDTH, RET_WIDTH,
            MLA_Q_RANK, MLA_KV_RANK, ROPE_DIM)
IN_WIDTH = sum(IN_SIZES)
D_FF = 4 * D_MODEL
CHUNK = 128
Q_BLOCK = 128
ROPE_THETA = 10000.0
EPS = 1e-6

kernel_name = 'hybrid_mlstm_retention_mla_encoder'


def _split_points():
    pts, acc = [], 0
    for s in IN_SIZES[:-1]:
        acc += s
        pts.append(acc)
    return pts


def _rms(x, g):
    xf = x.astype(jnp.float32)
    y = xf * lax.rsqrt(jnp.mean(xf * xf, axis=-1, keepdims=True) + EPS)
    return (y * g.astype(jnp.float32)).astype(x.dtype)


def _head_rms(h, g):
    B, S, H, d = h.shape
    hf = h.astype(jnp.float32)
    y = hf * lax.rsqrt(jnp.mean(hf * hf, axis=-1, keepdims=True) + EPS)
    return y.reshape(B, S, H * d) * g.astype(jnp.float32)


def _rope_tables(positions):
    half = ROPE_DIM // 2
    inv = ROPE_THETA ** (-jnp.arange(half, dtype=jnp.float32) / half)
    ang = positions.astype(jnp.float32)[..., None] * inv
    return jnp.cos(ang)[:, :, None, :], jnp.sin(ang)[:, :, None, :]


def _apply_rope(x, cos, sin):
    x1, x2 = jnp.split(x.astype(jnp.float32), 2, axis=-1)
    return jnp.concatenate([x1 * cos - x2 * sin, x2 * cos + x1 * sin], axis=-1).astype(x.dtype)


def _centred_conv(x, w):
    K = w.shape[0]
    pad = K // 2
    S = x.shape[1]
    xp = jnp.pad(x, ((0, 0), (pad, pad), (0, 0)))
    return sum(xp[:, j:j + S] * w[j] for j in range(K))


def _mlstm_chunkwise(q, k, v, log_i, log_f):
    B, H, S, d = q.shape
    nc = S // CHUNK
    qc = (q * d ** -0.5).reshape(B, H, nc, CHUNK, d)
    kc = k.reshape(B, H, nc, CHUNK, d)
    vc = v.reshape(B, H, nc, CHUNK, d)
    lic = log_i.reshape(B, H, nc, CHUNK)
    b = jnp.cumsum(log_f.reshape(B, H, nc, CHUNK), axis=-1)
    b_last = b[..., -1]
    w_end = b_last[..., None] - b + lic
    m_loc = jnp.max(w_end, axis=-1)
    e_end = jnp.exp(w_end - m_loc[..., None])
    c_loc = jnp.einsum('bhnld,bhnle,bhnl->bhnde', kc, vc, e_end)
    n_loc = jnp.einsum('bhnld,bhnl->bhnd', kc, e_end)

    def step(carry, inp):
        c, n, m = carry
        c_l, n_l, m_l, b_l = inp
        m_new = jnp.maximum(b_l + m, m_l)
        a = jnp.exp(b_l + m - m_new)
        g = jnp.exp(m_l - m_new)
        c_new = a[..., None, None] * c + g[..., None, None] * c_l
        n_new = a[..., None] * n + g[..., None] * n_l
        return (c_new, n_new, m_new), (c, n, m)

    init = (jnp.zeros((B, H, d, d), jnp.float32), jnp.zeros((B, H, d), jnp.float32),
            jnp.full((B, H), NEG_INIT, jnp.float32))
    xs = (jnp.moveaxis(c_loc, 2, 0), jnp.moveaxis(n_loc, 2, 0),
          jnp.moveaxis(m_loc, 2, 0), jnp.moveaxis(b_last, 2, 0))
    _, (c_st, n_st, m_st) = lax.scan(step, init, xs)
    c_st = jnp.moveaxis(c_st, 0, 2)
    n_st = jnp.moveaxis(n_st, 0, 2)
    m_st = jnp.moveaxis(m_st, 0, 2)

    mask = jnp.tril(jnp.ones((CHUNK, CHUNK), dtype=bool))
    d_log = jnp.where(mask, b[..., :, None] - b[..., None, :] + lic[..., None, :], -jnp.inf)
    inter_log = b + m_st[..., None]
    m_t = jnp.maximum(jnp.max(d_log, axis=-1), inter_log)
    w_intra = jnp.exp(d_log - m_t[..., None])
    w_inter = jnp.exp(inter_log - m_t)
    s = jnp.einsum('bhnld,bhnsd->bhnls', qc, kc) * w_intra
    num = (jnp.einsum('bhnls,bhnse->bhnle', s, vc)
           + w_inter[..., None] * jnp.einsum('bhnld,bhnde->bhnle', qc, c_st))
    den = jnp.sum(s, axis=-1) + w_inter * jnp.einsum('bhnld,bhnd->bhnl', qc, n_st)
    h = num / jnp.maximum(jnp.abs(den), jnp.exp(-m_t))[..., None]
    return h.reshape(B, H, S, d)


def _retention_chunkwise(q, k, v, log_gamma):
    B, H, S, dk = q.shape
    dv = v.shape[-1]
    nc = S // CHUNK
    qc = q.reshape(B, H, nc, CHUNK, dk)
    kc = k.reshape(B, H, nc, CHUNK, dk)
    vc = v.reshape(B, H, nc, CHUNK, dv)
    idx = jnp.arange(CHUNK, dtype=jnp.float32)
    diff = idx[:, None] - idx[None, :]
    lower = diff >= 0
    decay = jnp.where(lower[None], jnp.exp(jnp.where(lower, diff, 0.0)[None] * log_gamma[:, None, None]), 0.0)
    scores = jnp.einsum('bhnld,bhnsd->bhnls', qc, kc) * decay[None, :, None]
    y_intra = jnp.einsum('bhnls,bhnse->bhnle', scores, vc)
    zeta = jnp.exp((CHUNK - 1.0 - idx)[None, :] * log_gamma[:, None])
    r_loc = jnp.einsum('bhnld,bhnle,hl->bhnde', kc, vc, zeta)
    chunk_decay = jnp.exp(CHUNK * log_gamma)[None, :, None, None]

    def step(r, r_l):
        return chunk_decay * r + r_l, r

    _, r_start = lax.scan(step, jnp.zeros((B, H, dk, dv), jnp.float32), jnp.moveaxis(r_loc, 2, 0))
    r_start = jnp.moveaxis(r_start, 0, 2)
    inner = jnp.exp((idx + 1.0)[None, :] * log_gamma[:, None])
    y_inter = jnp.einsum('bhnld,bhnde->bhnle', qc, r_start) * inner[None, :, None, :, None]
    return (y_intra + y_inter).reshape(B, H, S, dv)


def _mlstm_group(q, k, v, o, gates, b_gates, w_conv, g_out):
    B, S, _ = q.shape
    qk = jax.nn.silu(_centred_conv(jnp.concatenate([q, k], axis=-1), w_conv))
    q, k = jnp.split(qk, 2, axis=-1)

    def to_heads(t):
        return t.astype(jnp.float32).reshape(B, S, ML_HEADS, ML_DIM).transpose(0, 2, 1, 3)

    qh, kh, vh = to_heads(q), to_heads(k), to_heads(v)
    gp = (gates + b_gates).astype(jnp.float32).reshape(B, S, 4, ML_HEADS).transpose(2, 0, 3, 1)
    i_fwd, f_fwd, i_bwd, f_bwd = gp[0], gp[1], gp[2], gp[3]

    def flip(t):
        return jnp.flip(t, axis=2)

    h_fwd = _mlstm_chunkwise(qh, kh, vh, i_fwd, jax.nn.log_sigmoid(f_fwd))
    h_bwd = flip(_mlstm_chunkwise(flip(qh), flip(kh), flip(vh), flip(i_bwd), jax.nn.log_sigmoid(flip(f_bwd))))
    h = (h_fwd + h_bwd).transpose(0, 2, 1, 3)
    return (jax.nn.sigmoid(o.astype(jnp.float32)) * _head_rms(h, g_out)).astype(v.dtype)


def _retention_group(q, k, v, g, cos, sin, g_out):
    B, S, _ = q.shape
    qh = _apply_rope(q.reshape(B, S, RET_HEADS, RET_QK), cos, sin).astype(jnp.float32).transpose(0, 2, 1, 3)
    kh = (_apply_rope(k.reshape(B, S, RET_HEADS, RET_QK), cos, sin).astype(jnp.float32)
          * RET_QK ** -0.5).transpose(0, 2, 1, 3)
    vh = v.astype(jnp.float32).reshape(B, S, RET_HEADS, RET_V).transpose(0, 2, 1, 3)
    log_gamma = jnp.log1p(-jnp.exp2(-RET_DECAY_BASE - jnp.arange(RET_HEADS, dtype=jnp.float32)))

    def flip(t):
        return jnp.flip(t, axis=2)

    y = (_retention_chunkwise(qh, kh, vh, log_gamma)
         + flip(_retention_chunkwise(flip(qh), flip(kh), flip(vh), log_gamma[::-1])))
    y = y.transpose(0, 2, 1, 3)
    return (jax.nn.silu(g.astype(jnp.float32)) * _head_rms(y, g_out)).astype(v.dtype)


def _mla_group(c_q, c_kv, k_rope, cos, sin, g_q_norm, w_q_up, g_kv_norm, w_kv_up):
    B, S, _ = c_q.shape
    q = (_rms(c_q, g_q_norm) @ w_q_up).reshape(B, S, MLA_HEADS, MLA_NOPE + ROPE_DIM)
    q = jnp.concatenate([q[..., :MLA_NOPE], _apply_rope(q[..., MLA_NOPE:], cos, sin)], axis=-1)
    kv = (_rms(c_kv, g_kv_norm) @ w_kv_up).reshape(B, S, MLA_HEADS, MLA_NOPE + MLA_V)
    k_nope, v = kv[..., :MLA_NOPE], kv[..., MLA_NOPE:]
    k_r = _apply_rope(k_rope[:, :, None, :], cos, sin)
    k = jnp.concatenate([k_nope, jnp.broadcast_to(k_r, (B, S, MLA_HEADS, ROPE_DIM))], axis=-1)
    scale = (MLA_NOPE + ROPE_DIM) ** -0.5
    nb = S // Q_BLOCK
    q_blocks = q.reshape(B, nb, Q_BLOCK, MLA_HEADS, MLA_NOPE + ROPE_DIM).transpose(1, 0, 2, 3, 4)

    def attend(qb):
        s = jnp.einsum('bqhd,bkhd->bhqk', qb, k).astype(jnp.float32) * scale
        p = jax.nn.softmax(s, axis=-1)
        return jnp.einsum('bhqk,bkhd->bqhd', p.astype(v.dtype), v)

    o = lax.map(attend, q_blocks)
    return o.transpose(1, 0, 2, 3, 4).reshape(B, S, MLA_WIDTH)


def _mixer(h, cos, sin, w_in, b_gates, w_conv, g_ml_out, g_ret_out, g_q_norm, w_q_up,
           g_kv_norm, w_kv_up, w_out):
    proj = h @ w_in
    (ml_q, ml_k, ml_v, ml_o, ml_gates, r_q, r_k, r_v, r_g,
     c_q, c_kv, k_rope) = jnp.split(proj, _split_points(), axis=-1)
    y_ml = _mlstm_group(ml_q, ml_k, ml_v, ml_o, ml_gates, b_gates, w_conv, g_ml_out)
    y_ret = _retention_group(r_q, r_k, r_v, r_g, cos, sin, g_ret_out)
    y_mla = _mla_group(c_q, c_kv, k_rope, cos, sin, g_q_norm, w_q_up, g_kv_norm, w_kv_up)
    y = jnp.concatenate([y_ml.astype(h.dtype), y_ret.astype(h.dtype), y_mla.astype(h.dtype)], axis=-1)
    return y @ w_out


def setup_inputs(seed: int = 0) -> dict:
    key = jax.random.key(seed)
    ks = jax.random.split(key, 20)

    def nrm(k, shape, scale):
        return jax.random.normal(k, shape, jnp.float32) * scale

    def gain(k, shape):
        return 1.0 + 0.02 * jax.random.normal(k, shape, jnp.float32)

    x = nrm(ks[0], (BATCH, SEQ, D_MODEL), 1.0)
    positions = (jax.random.randint(ks[1], (BATCH, 1), 0, 1024, dtype=jnp.int32)
                 + jnp.arange(SEQ, dtype=jnp.int32)[None, :]).astype(jnp.int32)
    forget = jnp.linspace(FORGET_BIAS_LO, FORGET_BIAS_HI, ML_HEADS, dtype=jnp.float32)
    zeros = jnp.zeros((ML_HEADS,), jnp.float32)
    b_gates = jnp.concatenate([zeros, forget, zeros, forget])[None, :] + nrm(ks[2], (DEPTH, 4 * ML_HEADS), 0.1)
    return {
        'x': x,
        'positions': positions,
        'g_mix': gain(ks[3], (DEPTH, D_MODEL)),
        'w_in': nrm(ks[4], (DEPTH, D_MODEL, IN_WIDTH), D_MODEL ** -0.5),
        'b_gates': b_gates,
        'w_conv': nrm(ks[5], (DEPTH, ML_CONV, 2 * ML_WIDTH), ML_CONV ** -0.5),
        'g_ml_out': gain(ks[6], (DEPTH, ML_WIDTH)),
        'g_ret_out': gain(ks[7], (DEPTH, RET_WIDTH)),
        'g_q_norm': gain(ks[8], (DEPTH, MLA_Q_RANK)),
        'w_q_up': nrm(ks[9], (DEPTH, MLA_Q_RANK, MLA_HEADS * (MLA_NOPE + ROPE_DIM)), MLA_Q_RANK ** -0.5),
        'g_kv_norm': gain(ks[10], (DEPTH, MLA_KV_RANK)),
        'w_kv_up': nrm(ks[11], (DEPTH, MLA_KV_RANK, MLA_HEADS * (MLA_NOPE + MLA_V)), MLA_KV_RANK ** -0.5),
        'w_out': nrm(ks[12], (DEPTH, MIX_WIDTH, D_MODEL), MIX_WIDTH ** -0.5),
        'g_ffn': gain(ks[13], (DEPTH, D_MODEL)),
        'w_ff1': nrm(ks[14], (DEPTH, D_MODEL, D_FF), D_MODEL ** -0.5),
        'w_ff2': nrm(ks[15], (DEPTH, D_FF, D_MODEL), D_FF ** -0.5),
        'g_final': gain(ks[16], (D_MODEL,)),
    }


def reference(x, positions, g_mix, w_in, b_gates, w_conv, g_ml_out, g_ret_out, g_q_norm, w_q_up,
              g_kv_norm, w_kv_up, w_out, g_ffn, w_ff1, w_ff2, g_final):
    cos, sin = _rope_tables(positions)
    for l in range(DEPTH):
        h = _rms(x, g_mix[l])
        x = x + _mixer(h, cos, sin, w_in[l], b_gates[l], w_conv[l], g_ml_out[l], g_ret_out[l],
                       g_q_norm[l], w_q_up[l], g_kv_norm[l], w_kv_up[l], w_out[l])
        u = _rms(x, g_ffn[l])
        x = x + jnp.square(jax.nn.relu(u @ w_ff1[l])) @ w_ff2[l]
    return _rms(x, g_final)
```

```python
import numpy as np
import ml_dtypes
import concourse.bass as bass
import concourse.mybir as mybir
from concourse.bass_utils import run_bass_kernel_spmd

F32 = mybir.dt.float32
BF16 = mybir.dt.bfloat16
I32 = mybir.dt.int32
AF = mybir.ActivationFunctionType
ALU = mybir.AluOpType
AX = mybir.AxisListType

COMPUTE = ("pe", "act", "dve", "pool")


class Prog:
    def __init__(self):
        self.nc = bass.Bass("TRN2", target_bir_lowering=False)
        self.ops = []
        self.dma_groups = {}
        self.out_groups = set()

    def op(self, eng, fn, reads=(), writes=(), acc=False):
        self.ops.append(dict(eng=eng, fn=fn, reads=tuple(reads), writes=tuple(writes),
                             dma=None, acc=acc))

    def dma(self, queue, out, in_, reads=(), writes=(), group=None, is_output=False, **kw):
        assert group is not None
        self.dma_groups.setdefault(group, 0)
        if is_output:
            self.out_groups.add(group)
        self.ops.append(dict(eng=queue, fn=lambda e: e.dma_start(out=out, in_=in_, **kw),
                             reads=tuple(reads), writes=tuple(writes), dma=group, acc=False))

    def emit(self):
        nc = self.nc
        ops = self.ops
        last_writer = {}
        readers = {}
        eng_pos = {}
        for i, o in enumerate(ops):
            e = o["eng"]
            o["pos"] = eng_pos.get(e, 0)
            eng_pos[e] = o["pos"] + 1
            if o["dma"] is not None:
                self.dma_groups[o["dma"]] += 16
                o["dma_count"] = self.dma_groups[o["dma"]]
        seen = {}
        signal = set()
        for i, o in enumerate(ops):
            deps = set()
            for k in o["reads"]:
                if k in last_writer:
                    deps.add(last_writer[k])
            for k in o["writes"]:
                if k in last_writer:
                    deps.add(last_writer[k])
                for r in readers.get(k, ()):
                    deps.add(r)
            deps.discard(i)
            e = o["eng"]
            sv = seen.setdefault(e, {})
            need = {}
            for d in deps:
                od = ops[d]
                if od["dma"] is not None:
                    src = ("dma", od["dma"])
                    val = od["dma_count"]
                else:
                    if od["eng"] == e and e == "pe":
                        continue
                    src = ("eng", od["eng"])
                    val = od["pos"]
                if sv.get(src, -1) >= val:
                    continue
                if need.get(src, (-1, None))[0] < val:
                    need[src] = (val, d)
            o["waits"] = []
            for src, (val, d) in need.items():
                sv[src] = val
                o["waits"].append((src, d))
                if src[0] == "eng":
                    signal.add(d)
            for k in o["reads"]:
                readers.setdefault(k, []).append(i)
            for k in o["writes"]:
                last_writer[k] = i
                readers[k] = []
        seqc = {}
        for i, o in enumerate(ops):
            if o["dma"] is None and i in signal:
                seqc[o["eng"]] = seqc.get(o["eng"], 0) + 1
                o["seq"] = seqc[o["eng"]]
        self.n_signal = dict(seqc)

        import contextlib
        with contextlib.ExitStack() as st:
            sems = {}
            for e in COMPUTE + ("sp",):
                sems[("eng", e)] = st.enter_context(nc.semaphore("s_" + e))
            for g in self.dma_groups:
                sems[("dma", g)] = st.enter_context(nc.semaphore("d_" + g))
            block = st.enter_context(nc.Block())
            by_eng = {}
            for i, o in enumerate(ops):
                by_eng.setdefault(o["eng"], []).append(i)

            def run(engname, eobj):
                for i in by_eng.get(engname, []):
                    o = ops[i]
                    for src, d in o["waits"]:
                        od = ops[d]
                        val = od["dma_count"] if src[0] == "dma" else od["seq"]
                        eobj.wait_ge(sems[src], val)
                    ins = o["fn"](eobj)
                    if o["dma"] is not None:
                        ins.then_inc(sems[("dma", o["dma"])], 16)
                    elif i in signal:
                        ins.then_inc(sems[("eng", engname)], 1)
                if engname == "sp":
                    for g in sorted(self.out_groups):
                        eobj.wait_ge(sems[("dma", g)], self.dma_groups[g])

            @block.sync
            def _(e):
                run("sp", e)

            @block.tensor
            def _(e):
                run("pe", e)

            @block.scalar
            def _(e):
                run("act", e)

            @block.vector
            def _(e):
                run("dve", e)

            @block.gpsimd
            def _(e):
                run("pool", e)
        return nc


S_FULL, DM, NCORE = 8192, 2048, 8
TPC = S_FULL // NCORE
KC = DM // 128
SEC = dict(mq=0, mk=512, mv=1024, mo=1536, gates=2048, rq=2064, rk=2320, rv=2576, rg=3088,
           cq=3600, ckv=4112, kr=4368)
IN_W = 4432
EPS = 1e-6


class Ring:
    def __init__(self, nc, name, n, shape, dtype, psum=False):
        self.n = n
        self.i = 0
        self.name = name
        if psum:
            self.t = [nc.alloc_psum_tensor(f"{name}{i}", shape, dtype).ap() for i in range(n)]
        else:
            self.t = [nc.alloc_sbuf_tensor(f"{name}{i}", shape, dtype).ap() for i in range(n)]

    def next(self):
        j = self.i % self.n
        self.i += 1
        return self.t[j], f"{self.name}{j}"


def fm_rmsnorm(P, src_fn, nchunk, T, gcol, gkey, out, out_keys, ones, psum3, sq_ring, rstd, tmp, epst, inv_n, out_fn=None, post=None):
    tiles = [(a, min(a + 512, T)) for a in range(0, T, 512)]
    for c in range(nchunk):
        src, skey = src_fn(c)
        sq, sqk = sq_ring.next()
        P.op("act", lambda e, sq=sq, src=src: e.activation(out=sq[:, 0:T], in_=src, func=AF.Square),
             reads=[skey], writes=[sqk])
        for ti, (a, b) in enumerate(tiles):
            P.op("pe", lambda e, sq=sq, a=a, b=b, ti=ti, c=c: e.matmul(psum3[ti][0][:, 0:b - a], lhsT=ones, rhs=sq[:, a:b],
                                                                    start=(c == 0), stop=(c == nchunk - 1)),
                 reads=[sqk, "ones"], writes=[psum3[ti][1]])
    for ti, (a, b) in enumerate(tiles):
        P.op("act", lambda e, a=a, b=b, ti=ti: e.activation(out=tmp[:, a:b], in_=psum3[ti][0][:, 0:b - a], func=AF.Sqrt,
                                                           bias=epst[:, 0:1], scale=inv_n),
             reads=[psum3[ti][1], "epst"], writes=["nrm_tmp"])
        P.op("dve", lambda e, a=a, b=b: e.reciprocal(out=rstd[:, a:b], in_=tmp[:, a:b]), reads=["nrm_tmp"], writes=["nrm_rstd"])
    for c in range(nchunk):
        src, skey = src_fn(c)
        if out_fn is not None:
            oc, ock = out_fn(c)
        else:
            oc, ock = out[c], out_keys[c]
        P.op("dve", lambda e, c=c, src=src, oc=oc: e.scalar_tensor_tensor(out=oc, in0=src, scalar=gcol[c], in1=rstd[:, 0:T],
                                                                         op0=ALU.mult, op1=ALU.mult),
             reads=[skey, gkey, "nrm_rstd"], writes=[ock])
        if post is not None:
            post(c, oc, ock)


def _io(nc):
    def din(name, shape, dt=F32):
        return nc.dram_tensor(name, list(shape), dt, kind="ExternalInput").ap()

    def dout(name, shape, dt=F32):
        return nc.dram_tensor(name, list(shape), dt, kind="ExternalOutput").ap()

    def sb(name, shape, dt=F32):
        return nc.alloc_sbuf_tensor(name, list(shape), dt).ap()
    return din, dout, sb


def mm_fm(P, lhs, lhs_key, M, rhs, rhs_key, nk, ttiles, pset):
    for ti, (a, b) in enumerate(ttiles):
        for kc in range(nk):
            P.op("pe", lambda e, ti=ti, a=a, b=b, kc=kc: e.matmul(pset[ti][0][0:M, 0:b - a], lhsT=lhs(kc), rhs=rhs(kc, a, b),
                                                                 start=(kc == 0), stop=(kc == nk - 1)),
                 reads=[lhs_key, rhs_key], writes=[pset[ti][1]])


def build_L1():
    P = Prog()
    nc = P.nc
    din, dout, sb = _io(nc)
    T = TPC
    TH = T + 2
    xT_d = din("xT", [128, KC, TH])
    w_in = din("w_in", [128, KC, IN_W])
    w_sw = din("w_sw", [128, KC, 576])
    gmix_d = din("gmix", [128, KC])
    wconv_d = din("wconv", [128, 8, 3])
    pos_d = din("pos", [1, T], I32)
    invf_d = din("invf", [64, 2])
    gq_d = din("gq", [128, 4])
    gkv_d = din("gkv", [128, 2])
    wq_d = din("wq", [128, 4, 1536])
    wqsw_d = din("wq_sw", [128, 4, 512])
    wkv_d = din("wkv", [128, 2, 2048])
    o_mq = dout("o_mq", [128, 4, T], BF16)
    o_mk = dout("o_mk", [128, 4, T], BF16)
    o_mv = dout("o_mv", [128, 4, T], BF16)
    o_mo = dout("o_mo", [128, 4, T], F32)
    o_gates = dout("o_gates", [16, T], F32)
    o_rq = dout("o_rq", [64, 4, T], BF16)
    o_rk = dout("o_rk", [64, 4, T], BF16)
    o_rv = dout("o_rv", [128, 4, T], BF16)
    o_rg = dout("o_rg", [128, 4, T], F32)
    o_qn = dout("o_qn", [128, 8, T], BF16)
    o_qr = dout("o_qr", [64, 8, T], BF16)
    o_kn = dout("o_kn", [128, 8, T], BF16)
    o_v = dout("o_v", [128, 8, T], BF16)
    o_kr = dout("o_kr", [64, T], BF16)

    xr_ring = Ring(nc, "xr", 3, [128, TH], F32)
    hT = sb("hT", [128, KC, TH], BF16)
    ones = sb("ones", [128, 128])
    epst = sb("epst", [128, 1])
    gmix = sb("gmixs", [128, KC])
    wconv = sb("wconvs", [128, 8, 3])
    invf = sb("invfs", [64, 2])
    gq = sb("gqs", [128, 4])
    gkv = sb("gkvs", [128, 2])
    wq = sb("wqs", [128, 4, 1536], BF16)
    wqsw = sb("wqsws", [128, 4, 512], BF16)
    wkv = sb("wkvs", [128, 2, 2048], BF16)
    rstd = sb("rstd", [128, TH])
    ntmp = sb("ntmp", [128, TH])
    sq_ring = Ring(nc, "sq", 2, [128, TH], F32)
    pre_ring = Ring(nc, "pre", 2, [128, TH], F32)
    cv_ring = Ring(nc, "cv", 2, [128, T], F32)
    ob_ring = Ring(nc, "ob", 3, [128, T], BF16)
    of_ring = Ring(nc, "of", 2, [128, T], F32)
    w_ring = Ring(nc, "w", 2, [128, KC, 256], BF16)
    wsm = sb("wsm", [128, KC, 144], BF16)
    cq = sb("cq", [128, 4, T])
    ckv = sb("ckv", [128, 2, T])
    cqn = sb("cqn", [128, 4, T], BF16)
    ckvn = sb("ckvn", [128, 2, T], BF16)
    posi = cv_ring.t[0][0:64, 0:T].bitcast(I32)
    ang = sq_ring.t[0][0:64, 0:T]
    ang2 = sq_ring.t[1][0:64, 0:T]
    kf = pre_ring.t[0][0:64, 0:T]
    fx = pre_ring.t[1][0:64, 0:T]
    cosT = sb("cosT", [64, T])
    sinT = sb("sinT", [64, T])
    rt1 = sb("rt1", [64, T])
    rt2 = sb("rt2", [64, T])
    banks = [nc.alloc_psum_tensor(f"pb{i}", [128, 512], F32).ap() for i in range(8)]
    psets = [[(banks[0], "pb0"), (banks[1], "pb1"), (banks[2], "pb2")],
             [(banks[3], "pb3"), (banks[4], "pb4"), (banks[5], "pb5")]]
    pctr = [0]

    def next_pset():
        pctr[0] += 1
        return psets[pctr[0] % 2]

    for (t, dsrc, key) in ((gmix, gmix_d, "gmix"), (wconv, wconv_d, "wconv"), (invf, invf_d, "invf"), (gq, gq_d, "gq"), (gkv, gkv_d, "gkv")):
        P.dma("sp", t, dsrc, writes=[key], group="ldc_" + key)
    P.dma("sp", posi, pos_d.partition_broadcast(64), writes=["cv0"], group="ldc_pos")
    P.dma("pool", wq, wq_d, writes=["wq"], group="ldc_wq")
    P.dma("pool", wqsw, wqsw_d, writes=["wqsw"], group="ldc_wqsw")
    P.dma("pool", wkv, wkv_d, writes=["wkv"], group="ldc_wkv")
    P.op("dve", lambda e: e.memset(ones, 1.0), writes=["ones"])
    P.op("dve", lambda e: e.memset(epst, EPS), writes=["epst"])

    def range_reduce(src, dst, key_src, key_dst):
        P.op("dve", lambda e: e.tensor_scalar(out=kf, in0=src, scalar1=float(1.0 / (2 * np.pi)), scalar2=None, op0=ALU.mult),
             reads=[key_src], writes=["pre0"])
        P.op("dve", lambda e: e.tensor_copy(out=posi, in_=kf), reads=["pre0"], writes=["cv0"])
        P.op("dve", lambda e: e.tensor_copy(out=kf, in_=posi), reads=["cv0"], writes=["pre0"])
        P.op("dve", lambda e: e.scalar_tensor_tensor(out=dst, in0=kf, scalar=float(-2 * np.pi), in1=src, op0=ALU.mult, op1=ALU.add),
             reads=["pre0", key_src], writes=[key_dst])
        P.op("dve", lambda e: e.tensor_scalar(out=fx, in0=dst, scalar1=float(np.pi), scalar2=float(-2 * np.pi), op0=ALU.is_gt, op1=ALU.mult),
             reads=[key_dst], writes=["pre1"])
        P.op("dve", lambda e: e.tensor_tensor(out=dst, in0=dst, in1=fx, op=ALU.add), reads=[key_dst, "pre1"], writes=[key_dst])
        P.op("dve", lambda e: e.tensor_scalar(out=fx, in0=dst, scalar1=float(-np.pi), scalar2=float(2 * np.pi), op0=ALU.is_lt, op1=ALU.mult),
             reads=[key_dst], writes=["pre1"])
        P.op("dve", lambda e: e.tensor_tensor(out=dst, in0=dst, in1=fx, op=ALU.add), reads=[key_dst, "pre1"], writes=[key_dst])

    P.op("dve", lambda e: e.tensor_copy(out=ang, in_=posi), reads=["cv0"], writes=["sq0"])
    P.op("dve", lambda e: e.tensor_scalar(out=ang, in0=ang, scalar1=invf[:, 0:1], scalar2=None, op0=ALU.mult),
         reads=["sq0", "invf"], writes=["sq0"])
    P.op("dve", lambda e: e.tensor_scalar(out=ang2, in0=ang, scalar1=float(np.pi / 2), scalar2=None, op0=ALU.add),
         reads=["sq0"], writes=["sq1"])
    range_reduce(ang, rt1, "sq0", "rt1")
    P.op("act", lambda e: e.activation(out=sinT, in_=rt1, func=AF.Sin), reads=["rt1"], writes=["sinT"])
    range_reduce(ang2, rt2, "sq1", "rt2")
    P.op("act", lambda e: e.activation(out=cosT, in_=rt2, func=AF.Sin), reads=["rt2"], writes=["cosT"])
    P.op("dve", lambda e: e.tensor_scalar(out=sinT, in0=sinT, scalar1=invf[:, 1:2], scalar2=None, op0=ALU.mult),
         reads=["sinT", "invf"], writes=["sinT"])

    ps = next_pset()
    def fetch_x(kc):
        xt, xk = xr_ring.next()
        P.dma("sp", xt, xT_d[:, kc, :], writes=[xk], group="ld_" + xk)
        return xt, xk

    fm_rmsnorm(P, fetch_x, KC, TH,
               [gmix[:, kc:kc + 1] for kc in range(KC)], "gmix", [hT[:, kc, :] for kc in range(KC)], ["hT"] * KC,
               ones, ps, sq_ring, rstd, ntmp, epst, 1.0 / DM)

    tt_own = [(0, 512), (512, 1024)]
    tt_halo = [(0, 512), (512, 1024), (1024, TH)]

    def rhs_own(kc, a, b):
        return hT[:, kc, 1 + a:1 + b]

    def rhs_halo(kc, a, b):
        return hT[:, kc, a:b]

    def load_w(src):
        wt, wk = w_ring.next()
        n = src.shape[2]
        P.dma("pool", wt[:, :, 0:n], src, writes=[wk], group="ld_" + wk)
        return wt, wk

    def store(dst, src, skey):
        P.dma("sp", dst, src, reads=[skey], group="st_" + skey, is_output=True)

    def rope_out(pa, pb, dst, scale=1.0):
        ob, obk = ob_ring.next()
        for ti, (a, b) in enumerate(tt_own):
            P.op("dve", lambda e, ti=ti, a=a, b=b: e.scalar_tensor_tensor(out=rt1[:, a:b], in0=pa[ti][0][0:64, 0:b - a], scalar=float(scale),
                                                                           in1=cosT[:, a:b], op0=ALU.mult, op1=ALU.mult),
                 reads=[pa[ti][1], "cosT"], writes=["rt1"])
            P.op("dve", lambda e, ti=ti, a=a, b=b: e.scalar_tensor_tensor(out=rt2[:, a:b], in0=pb[ti][0][0:64, 0:b - a], scalar=float(scale),
                                                                           in1=sinT[:, a:b], op0=ALU.mult, op1=ALU.mult),
                 reads=[pb[ti][1], "sinT"], writes=["rt2"])
        P.op("dve", lambda e: e.tensor_tensor(out=ob[0:64, :], in0=rt1, in1=rt2, op=ALU.add), reads=["rt1", "rt2"], writes=[obk])
        store(dst, ob[0:64, :], obk)

    for sec, dst, scale in (("mq", o_mq, 128 ** -0.5), ("mk", o_mk, None)):
        for j in range(4):
            if j % 2 == 0:
                wt, wk = load_w(w_in[:, :, SEC[sec] + j * 128:SEC[sec] + j * 128 + 256])
            blk = (0 if sec == "mq" else 4) + j
            ps = next_pset()
            mm_fm(P, lambda kc, j=j, wt=wt: wt[:, kc, (j % 2) * 128:(j % 2 + 1) * 128], wk, 128, rhs_halo, "hT", KC, tt_halo, ps)
            pre, prek = pre_ring.next()
            for ti, (a, b) in enumerate(tt_halo):
                P.op("act", lambda e, ti=ti, a=a, b=b, pre=pre, ps=ps: e.copy(out=pre[:, a:b], in_=ps[ti][0][:, 0:b - a]),
                     reads=[ps[ti][1]], writes=[prek])
            cv, cvk = cv_ring.next()
            P.op("dve", lambda e, cv=cv, pre=pre, blk=blk: e.tensor_scalar(out=cv, in0=pre[:, 0:T], scalar1=wconv[:, blk, 0:1], scalar2=None, op0=ALU.mult),
                 reads=[prek, "wconv"], writes=[cvk])
            P.op("dve", lambda e, cv=cv, pre=pre, blk=blk: e.scalar_tensor_tensor(out=cv, in0=pre[:, 1:T + 1], scalar=wconv[:, blk, 1:2], in1=cv, op0=ALU.mult, op1=ALU.add),
                 reads=[prek, "wconv", cvk], writes=[cvk])
            P.op("dve", lambda e, cv=cv, pre=pre, blk=blk: e.scalar_tensor_tensor(out=cv, in0=pre[:, 2:T + 2], scalar=wconv[:, blk, 2:3], in1=cv, op0=ALU.mult, op1=ALU.add),
                 reads=[prek, "wconv", cvk], writes=[cvk])
            ob, obk = ob_ring.next()
            if scale is None:
                P.op("act", lambda e, cv=cv, ob=ob: e.activation(out=ob, in_=cv, func=AF.Silu), reads=[cvk], writes=[obk])
            else:
                P.op("act", lambda e, cv=cv: e.activation(out=cv, in_=cv, func=AF.Silu), reads=[cvk], writes=[cvk])
                P.op("dve", lambda e, cv=cv, ob=ob, scale=scale: e.tensor_scalar(out=ob, in0=cv, scalar1=float(scale), scalar2=None, op0=ALU.mult),
                     reads=[cvk], writes=[obk])
            store(dst[:, j, :], ob, obk)

    def plain_section(sec, ncols, post):
        for j in range(ncols // 128):
            if j % 2 == 0:
                wt, wk = load_w(w_in[:, :, SEC[sec] + j * 128:SEC[sec] + j * 128 + 256])
            ps = next_pset()
            mm_fm(P, lambda kc, j=j, wt=wt: wt[:, kc, (j % 2) * 128:(j % 2 + 1) * 128], wk, 128, rhs_own, "hT", KC, tt_own, ps)
            post(j, ps)

    def post_bf16(dst):
        def f(j, ps):
            ob, obk = ob_ring.next()
            for ti, (a, b) in enumerate(tt_own):
                P.op("act", lambda e, ti=ti, a=a, b=b, ob=ob, ps=ps: e.copy(out=ob[:, a:b], in_=ps[ti][0][:, 0:b - a]),
                     reads=[ps[ti][1]], writes=[obk])
            store(dst[:, j, :], ob, obk)
        return f

    def post_act(dst, func):
        def f(j, ps):
            of, ofk = of_ring.next()
            for ti, (a, b) in enumerate(tt_own):
                P.op("act", lambda e, ti=ti, a=a, b=b, of=of, ps=ps: e.activation(out=of[:, a:b], in_=ps[ti][0][:, 0:b - a], func=func),
                     reads=[ps[ti][1]], writes=[ofk])
            store(dst[:, j, :], of, ofk)
        return f

    def post_keep(t, key):
        def f(j, ps):
            for ti, (a, b) in enumerate(tt_own):
                P.op("act", lambda e, ti=ti, a=a, b=b, ps=ps, j=j: e.copy(out=t[:, j, a:b], in_=ps[ti][0][:, 0:b - a]),
                     reads=[ps[ti][1]], writes=[f"{key}{j}"])
        return f

    plain_section("mv", 512, post_bf16(o_mv))
    plain_section("mo", 512, post_act(o_mo, AF.Sigmoid))
    plain_section("rv", 512, post_bf16(o_rv))
    plain_section("rg", 512, post_act(o_rg, AF.Silu))
    plain_section("cq", 512, post_keep(cq, "cq"))
    plain_section("ckv", 256, post_keep(ckv, "ckv"))

    for sec_i, (dst, scale) in enumerate(((o_rq, 1.0), (o_rk, 0.125))):
        wt, wk = load_w(w_in[:, :, SEC["rq"] + sec_i * 256:SEC["rq"] + sec_i * 256 + 256])
        ws, wsk = load_w(w_sw[:, :, sec_i * 256:sec_i * 256 + 256])
        for h in range(4):
            c0 = h * 64
            pa = next_pset()
            mm_fm(P, lambda kc, c0=c0, wt=wt: wt[:, kc, c0:c0 + 64], wk, 64, rhs_own, "hT", KC, tt_own, pa)
            pb = next_pset()
            mm_fm(P, lambda kc, c0=c0, ws=ws: ws[:, kc, c0:c0 + 64], wsk, 64, rhs_own, "hT", KC, tt_own, pb)
            rope_out(pa, pb, dst[:, h, :], scale)

    P.dma("pool", wsm[:, :, 0:16], w_in[:, :, SEC["gates"]:SEC["gates"] + 16], writes=["wsm_g"], group="ld_wsm0")
    P.dma("pool", wsm[:, :, 16:80], w_in[:, :, SEC["kr"]:SEC["kr"] + 64], writes=["wsm_k"], group="ld_wsm1")
    P.dma("pool", wsm[:, :, 80:144], w_sw[:, :, 512:576], writes=["wsm_ks"], group="ld_wsm2")
    ps = next_pset()
    mm_fm(P, lambda kc: wsm[:, kc, 0:16], "wsm_g", 16, rhs_own, "hT", KC, tt_own, ps)
    of, ofk = of_ring.next()
    for ti, (a, b) in enumerate(tt_own):
        P.op("act", lambda e, ti=ti, a=a, b=b, of=of, ps=ps: e.copy(out=of[0:16, a:b], in_=ps[ti][0][0:16, 0:b - a]),
             reads=[ps[ti][1]], writes=[ofk])
    store(o_gates, of[0:16, :], ofk)
    pa = next_pset()
    mm_fm(P, lambda kc: wsm[:, kc, 16:80], "wsm_k", 64, rhs_own, "hT", KC, tt_own, pa)
    pb = next_pset()
    mm_fm(P, lambda kc: wsm[:, kc, 80:144], "wsm_ks", 64, rhs_own, "hT", KC, tt_own, pb)
    rope_out(pa, pb, o_kr)

    ps = next_pset()
    fm_rmsnorm(P, lambda c: (cq[:, c, :], f"cq{c}"), 4, T, [gq[:, c:c + 1] for c in range(4)], "gq",
               [cqn[:, c, :] for c in range(4)], ["cqn"] * 4, ones, ps, sq_ring, rstd, ntmp, epst, 1.0 / 512)
    ps = next_pset()
    fm_rmsnorm(P, lambda c: (ckv[:, c, :], f"ckv{c}"), 2, T, [gkv[:, c:c + 1] for c in range(2)], "gkv",
               [ckvn[:, c, :] for c in range(2)], ["ckvn"] * 2, ones, ps, sq_ring, rstd, ntmp, epst, 1.0 / 256)

    def rhs_cq(kc, a, b):
        return cqn[:, kc, a:b]

    def rhs_ckv(kc, a, b):
        return ckvn[:, kc, a:b]

    pb16_qn = post_bf16(o_qn)
    pb16_kn = post_bf16(o_kn)
    pb16_v = post_bf16(o_v)
    for h in range(8):
        ps = next_pset()
        mm_fm(P, lambda kc, h=h: wq[:, kc, h * 192:h * 192 + 128], "wq", 128, rhs_cq, "cqn", 4, tt_own, ps)
        pb16_qn(h, ps)
        pa = next_pset()
        mm_fm(P, lambda kc, h=h: wq[:, kc, h * 192 + 128:h * 192 + 192], "wq", 64, rhs_cq, "cqn", 4, tt_own, pa)
        pb = next_pset()
        mm_fm(P, lambda kc, h=h: wqsw[:, kc, h * 64:(h + 1) * 64], "wqsw", 64, rhs_cq, "cqn", 4, tt_own, pb)
        rope_out(pa, pb, o_qr[:, h, :])
        ps = next_pset()
        mm_fm(P, lambda kc, h=h: wkv[:, kc, h * 256:h * 256 + 128], "wkv", 128, rhs_ckv, "ckvn", 2, tt_own, ps)
        pb16_kn(h, ps)
        ps = next_pset()
        mm_fm(P, lambda kc, h=h: wkv[:, kc, h * 256 + 128:h * 256 + 256], "wkv", 128, rhs_ckv, "ckvn", 2, tt_own, ps)
        pb16_v(h, ps)
    return P.emit()


def _kcp(w):
    K, N = w.shape
    return np.ascontiguousarray(w.reshape(K // 128, 128, N).transpose(1, 0, 2))


def _swap_halves(w, width=64):
    K, N = w.shape
    w4 = w.reshape(K, N // width, 2, width // 2)
    return np.ascontiguousarray(w4[:, :, ::-1, :].reshape(K, N))


def _rope_consts():
    inv = (10000.0 ** (-np.arange(32, dtype=np.float32) / 32)).astype(np.float32)
    t = np.zeros((64, 2), np.float32)
    t[:, 0] = np.concatenate([inv, inv])
    t[:, 1] = np.concatenate([-np.ones(32, np.float32), np.ones(32, np.float32)])
    return t


def l1_inputs(x_tm, positions, l, W):
    xpad = np.concatenate([np.zeros((1, DM), np.float32), x_tm, np.zeros((1, DM), np.float32)], 0)
    w_in = W["w_in"][l]
    shared = dict(
        w_in=_kcp(w_in),
        w_sw=_kcp(np.concatenate([_swap_halves(w_in[:, SEC["rq"]:SEC["rq"] + 256]), _swap_halves(w_in[:, SEC["rk"]:SEC["rk"] + 256]),
                                  _swap_halves(w_in[:, SEC["kr"]:SEC["kr"] + 64])], 1)),
        gmix=np.ascontiguousarray(W["g_mix"][l].reshape(KC, 128).T),
        wconv=np.ascontiguousarray(W["w_conv"][l].reshape(3, 8, 128).transpose(2, 1, 0)),
        invf=_rope_consts(),
        gq=np.ascontiguousarray(W["g_q_norm"][l].reshape(4, 128).T),
        gkv=np.ascontiguousarray(W["g_kv_norm"][l].reshape(2, 128).T),
        wq=_kcp(W["w_q_up"][l]),
        wq_sw=_kcp(_swap_halves(np.ascontiguousarray(W["w_q_up"][l].reshape(512, 8, 192)[:, :, 128:].reshape(512, 512)))),
        wkv=_kcp(W["w_kv_up"][l]),
    )
    maps = []
    for c in range(NCORE):
        seg = xpad[c * TPC:c * TPC + TPC + 2]
        m = dict(shared)
        m["xT"] = np.ascontiguousarray(seg.T.reshape(KC, 128, TPC + 2).transpose(1, 0, 2))
        m["pos"] = np.ascontiguousarray(positions[:, c * TPC:(c + 1) * TPC]).astype(np.int32)
        maps.append(m)
    return maps


def build_L2b():
    P = Prog()
    nc = P.nc
    din, dout, sb = _io(nc)
    S = S_FULL
    NB = S // 128
    qn_d = din("qn", [128, S], BF16)
    qr_d = din("qr", [64, S], BF16)
    kn_d = din("kn", [128, S], BF16)
    kr_d = din("kr", [64, S], BF16)
    vt_d = din("vt", [128, NB, 128], BF16)
    y_d = dout("yT", [128, S], BF16)
    qn = sb("qn_s", [128, S], BF16)
    qr = sb("qr_s", [64, S], BF16)
    kn = sb("kn_s", [128, S], BF16)
    kr = sb("kr_s", [64, S], BF16)
    vt = sb("vt_s", [128, NB, 128], BF16)
    ones = sb("ones_b", [128, 128], BF16)
    pt_ring = Ring(nc, "pt", 4, [128, 512], BF16)
    rec_ring = Ring(nc, "rec", 2, [128, 512], F32)
    y_ring = Ring(nc, "ys", 2, [128, 512], BF16)
    ps_s = Ring(nc, "pss", 3, [128, 512], F32, psum=True)
    ps_o = Ring(nc, "pso", 2, [128, 512], F32, psum=True)
    ps_d = Ring(nc, "psd", 2, [128, 512], F32, psum=True)
    for i in range(4):
        sl = slice(i * 2048, (i + 1) * 2048)
        P.dma("sp", kn[:, sl], kn_d[:, sl], writes=["kn"], group=f"ld_kn{i}")
        P.dma("sp", kr[:, sl], kr_d[:, sl], writes=["kr"], group=f"ld_kr{i}")
        P.dma("sp", qn[:, sl], qn_d[:, sl], writes=["qn"], group=f"ld_qn{i}")
        P.dma("sp", qr[:, sl], qr_d[:, sl], writes=["qr"], group=f"ld_qr{i}")
        P.dma("sp", vt[:, i * 16:(i + 1) * 16, :], vt_d[:, i * 16:(i + 1) * 16, :], writes=["vt"], group=f"ld_vt{i}")
    P.op("dve", lambda e: e.memset(ones, 1.0), writes=["ones"])
    scale = float(192 ** -0.5)
    for qt in range(S // 512):
        qs = slice(qt * 512, (qt + 1) * 512)
        po, pok = ps_o.next()
        pd, pdk = ps_d.next()
        pts = {}
        for i in range(NB + 2):
            if i < NB:
                ks = slice(i * 128, (i + 1) * 128)
                pss, pssk = ps_s.next()
                P.op("pe", lambda e, pss=pss, ks=ks, qs=qs: e.matmul(pss, lhsT=kn[:, ks], rhs=qn[:, qs], start=True, stop=False),
                     reads=["kn", "qn"], writes=[pssk])
                P.op("pe", lambda e, pss=pss, ks=ks, qs=qs: e.matmul(pss, lhsT=kr[:, ks], rhs=qr[:, qs], start=False, stop=True),
                     reads=["kr", "qr"], writes=[pssk])
                pt, ptk = pt_ring.next()
                P.op("act", lambda e, pss=pss, pt=pt: e.activation(out=pt, in_=pss, func=AF.Exp, scale=scale), reads=[pssk], writes=[ptk])
                pts[i] = (pt, ptk)
            if i >= 2:
                kb = i - 2
                pt, ptk = pts.pop(kb)
                P.op("pe", lambda e, pt=pt, kb=kb, po=po: e.matmul(po, lhsT=vt[:, kb, :], rhs=pt, start=(kb == 0), stop=(kb == NB - 1)),
                     reads=["vt", ptk], writes=[pok])
                P.op("pe", lambda e, pt=pt, kb=kb, pd=pd: e.matmul(pd, lhsT=ones, rhs=pt, start=(kb == 0), stop=(kb == NB - 1)),
                     reads=["ones", ptk], writes=[pdk])
        rec, reck = rec_ring.next()
        P.op("dve", lambda e, rec=rec, pd=pd: e.reciprocal(out=rec, in_=pd), reads=[pdk], writes=[reck])
        ys, ysk = y_ring.next()
        P.op("dve", lambda e, ys=ys, po=po, rec=rec: e.tensor_tensor(out=ys, in0=po, in1=rec, op=ALU.mult), reads=[pok, reck], writes=[ysk])
        P.dma("sp", y_d[:, qs], ys, reads=[ysk], group="st_" + ysk, is_output=True)
    return P.emit()


def build_L2a():
    P = Prog()
    nc = P.nc
    din, dout, sb = _io(nc)
    S = S_FULL
    NB = S // 128
    qT_d = din("m_qT", [128, S], BF16)
    kT_d = din("m_kT", [128, S], BF16)
    kM_d = din("m_kM", [128, NB, 128], BF16)
    v_d = din("m_v", [128, NB, 128], BF16)
    gi_d = din("m_gi", [128, NB])
    gf_d = din("m_gf", [128, NB])
    gb_d = din("m_gb", [128, 2])
    rqT_d = din("r_qT", [64, S], BF16)
    rkT_d = din("r_kT", [64, S], BF16)
    rkM_d = din("r_kM", [128, NB, 64], BF16)
    rv_d = din("r_v", [128, NB, 128], BF16)
    rtab_d = din("r_tab", [128, 4])
    U_d = din("U", [128, 128])
    hm_d = dout("h_ml", [128, NB, 128])
    hr_d = dout("h_ret", [128, NB, 128])

    qT = sb("qT", [128, S], BF16)
    kT = sb("kT", [128, S], BF16)
    kM = sb("kM", [128, NB, 128], BF16)
    v1 = sb("v1", [128, NB, 129], BF16)
    gi = sb("gi", [128, NB])
    gf = sb("gf", [128, NB])
    gb = sb("gb", [128, 2])
    rqT = sb("rqT", [64, S], BF16)
    rkT = sb("rkT", [64, S], BF16)
    rkM = sb("rkM", [128, NB, 64], BF16)
    rv = sb("rv", [128, NB, 128], BF16)
    rtab = sb("rtab", [128, 4])
    U = sb("U_s", [128, 128])
    ones = sb("ones_f", [128, 128])
    onec = sb("onec", [128, 1])
    sp = sb("sp", [128, NB])
    av = sb("av", [128, NB])
    fl = sb("fl", [128, NB])
    eB = sb("eB", [128, NB])
    Cm = sb("Cm", [128, 129])
    Cr = sb("Cr", [64, 128])
    sm_ring = Ring(nc, "sm", 3, [128, 128], BF16)
    va_ring = Ring(nc, "va", 3, [128, 129], BF16)
    cp_ring = Ring(nc, "cp", 3, [128, 129], BF16)
    dn_ring = Ring(nc, "dn", 3, [128, 4], F32)
    hm_ring = Ring(nc, "hms", 2, [128, 8, 128], F32)
    hr_ring = Ring(nc, "hrs", 2, [128, 8, 128], F32)
    ps_s = Ring(nc, "pss", 2, [128, 512], F32, psum=True)
    ps_o = Ring(nc, "pso", 3, [128, 512], F32, psum=True)
    ps_c = Ring(nc, "psc", 3, [128, 512], F32, psum=True)

    for i in range(4):
        sl = slice(i * 2048, (i + 1) * 2048)
        bl = slice(i * 16, (i + 1) * 16)
        P.dma("sp", qT[:, sl], qT_d[:, sl], writes=["qT"], group=f"ld_q{i}")
        P.dma("sp", kT[:, sl], kT_d[:, sl], writes=["kT"], group=f"ld_k{i}")
        P.dma("sp", kM[:, bl, :], kM_d[:, bl, :], writes=["kM"], group=f"ld_kM{i}")
        P.dma("sp", v1[:, bl, 0:128], v_d[:, bl, :], writes=["v1"], group=f"ld_v{i}")
        P.dma("sp", rqT[:, sl], rqT_d[:, sl], writes=["rqT"], group=f"ld_rq{i}")
        P.dma("sp", rkT[:, sl], rkT_d[:, sl], writes=["rkT"], group=f"ld_rk{i}")
        P.dma("sp", rkM[:, bl, :], rkM_d[:, bl, :], writes=["rkM"], group=f"ld_rkM{i}")
        P.dma("sp", rv[:, bl, :], rv_d[:, bl, :], writes=["rv"], group=f"ld_rv{i}")
    for t, dsrc, key in ((gi, gi_d, "gi"), (gf, gf_d, "gf"), (gb, gb_d, "gb"), (rtab, rtab_d, "rtab"), (U, U_d, "U")):
        P.dma("sp", t, dsrc, writes=[key], group="ldc_" + key)
    P.op("dve", lambda e: e.memset(ones, 1.0), writes=["ones"])
    P.op("dve", lambda e: e.memset(onec, 1.0), writes=["onec"])
    P.op("dve", lambda e: e.memset(Cm, 0.0), writes=["Cm"])
    P.op("dve", lambda e: e.memset(Cr, 0.0), writes=["Cr"])
    P.op("pool", lambda e: e.memset(v1[:, :, 128:129], 1.0), writes=["v1"])

    P.op("dve", lambda e: e.tensor_scalar(out=gi, in0=gi, scalar1=gb[:, 0:1], scalar2=None, op0=ALU.add), reads=["gi", "gb"], writes=["gi"])
    P.op("dve", lambda e: e.tensor_scalar(out=gf, in0=gf, scalar1=gb[:, 1:2], scalar2=None, op0=ALU.add), reads=["gf", "gb"], writes=["gf"])
    P.op("act", lambda e: e.activation(out=sp, in_=gf, func=AF.Exp, scale=-1.0), reads=["gf"], writes=["sp"])
    P.op("act", lambda e: e.activation(out=sp, in_=sp, func=AF.Ln, bias=onec[:, 0:1], scale=1.0), reads=["sp", "onec"], writes=["sp"])
    pcs, pcsk = ps_o.next()
    ptot, ptotk = ps_c.next()
    P.op("pe", lambda e: e.matmul(pcs[:, 0:NB], lhsT=U, rhs=sp, start=True, stop=True), reads=["U", "sp"], writes=[pcsk])
    P.op("pe", lambda e: e.matmul(ptot[:, 0:NB], lhsT=ones, rhs=sp, start=True, stop=True), reads=["ones", "sp"], writes=[ptotk])
    P.op("dve", lambda e: e.tensor_copy(out=eB, in_=ptot[:, 0:NB]), reads=[ptotk], writes=["eB"])
    P.op("dve", lambda e: e.tensor_tensor(out=fl, in0=pcs[:, 0:NB], in1=eB, op=ALU.subtract), reads=[pcsk, "eB"], writes=["fl"])
    P.op("dve", lambda e: e.tensor_tensor(out=av, in0=fl, in1=gi, op=ALU.add), reads=["fl", "gi"], writes=["av"])
    P.op("act", lambda e: e.activation(out=av, in_=av, func=AF.Exp), reads=["av"], writes=["av"])
    P.op("act", lambda e: e.activation(out=fl, in_=fl, func=AF.Exp), reads=["fl"], writes=["fl"])
    P.op("act", lambda e: e.activation(out=eB, in_=eB, func=AF.Exp, scale=-1.0), reads=["eB"], writes=["eB"])

    def chunk(n, dk, qTt, kTt, kMt, vsrc, a_ap, eB_ap, Cst, Ckey, qk, kk, kMk, vk, akey, ekey, ncol, norm, hst, hstk):
        cs = slice(n * 128, (n + 1) * 128)
        pss, pssk = ps_s.next()
        P.op("pe", lambda e: e.matmul(pss[:, 0:128], lhsT=kTt[0:dk, cs], rhs=qTt[0:dk, cs], start=True, stop=True), reads=[kk, qk], writes=[pssk])
        sm, smk = sm_ring.next()
        P.op("dve", lambda e: e.tensor_tensor(out=sm, in0=pss[:, 0:128], in1=U, op=ALU.mult), reads=[pssk, "U"], writes=[smk])
        va, vak = va_ring.next()
        P.op("dve", lambda e: e.tensor_scalar(out=va[:, 0:ncol], in0=vsrc(n), scalar1=a_ap(n), scalar2=None, op0=ALU.mult),
             reads=[vk, akey], writes=[vak])
        cp, cpk = cp_ring.next()
        P.op("dve", lambda e: e.tensor_scalar(out=cp[0:dk, 0:ncol], in0=Cst[0:dk, 0:ncol], scalar1=eB_ap(n, dk), scalar2=None, op0=ALU.mult),
             reads=[Ckey, ekey], writes=[cpk])
        pso, psok = ps_o.next()
        P.op("pe", lambda e: e.matmul(pso[:, 0:ncol], lhsT=sm, rhs=va[:, 0:ncol], start=True, stop=False), reads=[smk, vak], writes=[psok])
        P.op("pe", lambda e: e.matmul(pso[:, 0:ncol], lhsT=qTt[0:dk, cs], rhs=cp[0:dk, 0:ncol], start=False, stop=True), reads=[qk, cpk], writes=[psok])
        psc, psck = ps_c.next()
        P.op("pe", lambda e: e.matmul(psc[0:dk, 0:ncol], lhsT=kMt[:, n, 0:dk], rhs=va[:, 0:ncol], start=True, stop=True), reads=[kMk, vak], writes=[psck])
        P.op("dve", lambda e: e.scalar_tensor_tensor(out=Cst[0:dk, 0:ncol], in0=Cst[0:dk, 0:ncol], scalar=eB_ap(n, dk), in1=psc[0:dk, 0:ncol],
                                                     op0=ALU.mult, op1=ALU.add), reads=[Ckey, ekey, psck], writes=[Ckey])
        hslot = hst[:, n % 8, :]
        if norm:
            dn, dnk = dn_ring.next()
            P.op("act", lambda e: e.activation(out=dn[:, 0:1], in_=pso[:, 128:129], func=AF.Abs), reads=[psok], writes=[dnk])
            P.op("dve", lambda e: e.tensor_tensor(out=dn[:, 1:2], in0=dn[:, 0:1], in1=fl[:, n:n + 1], op=ALU.max), reads=[dnk, "fl"], writes=[dnk])
            P.op("dve", lambda e: e.reciprocal(out=dn[:, 2:3], in_=dn[:, 1:2]), reads=[dnk], writes=[dnk])
            P.op("act", lambda e: e.activation(out=hslot, in_=pso[:, 0:128], func=AF.Copy, scale=dn[:, 2:3]), reads=[psok, dnk], writes=[hstk])
        else:
            P.op("act", lambda e: e.activation(out=hslot, in_=pso[:, 0:128], func=AF.Copy, scale=rtab[:, 1:2]), reads=[psok, "rtab"], writes=[hstk])

    hm = hr = None
    for n in range(NB):
        if n % 8 == 0:
            hm, hmk = hm_ring.next()
            hr, hrk = hr_ring.next()
        chunk(n, 128, qT, kT, kM, lambda n: v1[:, n, :], lambda n: av[:, n:n + 1], lambda n, dk: eB[0:dk, n:n + 1], Cm, "Cm",
              "qT", "kT", "kM", "v1", "av", "eB", 129, True, hm, hmk)
        chunk(n, 64, rqT, rkT, rkM, lambda n: rv[:, n, :], lambda n: rtab[:, 0:1], lambda n, dk: rtab[0:dk, 2:3], Cr, "Cr",
              "rqT", "rkT", "rkM", "rv", "rtab", "rtab", 128, False, hr, hrk)
        if n % 8 == 7:
            g = n // 8
            P.dma("sp", hm_d[:, g * 8:(g + 1) * 8, :], hm, reads=[hmk], group="st_" + hmk, is_output=True)
            P.dma("sp", hr_d[:, g * 8:(g + 1) * 8, :], hr, reads=[hrk], group="st_" + hrk, is_output=True)
    return P.emit()


def _cat_tokens(results, name):
    return np.concatenate([np.asarray(r[name]) for r in results], axis=-1)


def _tm_blocks(a_ft):
    F_, S_ = a_ft.shape
    return np.ascontiguousarray(a_ft.T.reshape(S_ // 128, 128, F_).transpose(1, 0, 2))


_GAMMA = [1.0 - 2.0 ** (-5 - h) for h in range(4)]


def _ret_table(g):
    idx = np.arange(128, dtype=np.float64)
    t = np.zeros((128, 4), np.float32)
    t[:, 0] = g ** (127 - idx)
    t[:, 1] = g ** (-(127 - idx))
    t[:, 2] = g ** 128
    return t


def l2a_inputs(r1, l, W):
    mq = _cat_tokens(r1, "o_mq"); mk = _cat_tokens(r1, "o_mk"); mv = _cat_tokens(r1, "o_mv")
    gates = _cat_tokens(r1, "o_gates")
    rq = _cat_tokens(r1, "o_rq"); rk = _cat_tokens(r1, "o_rk"); rv = _cat_tokens(r1, "o_rv")
    U = np.triu(np.ones((128, 128), np.float32))
    bg = W["b_gates"][l]
    maps = []
    for j in range(NCORE):
        d, hd = j // 4, j % 4
        sl = slice(None, None, -1) if d == 1 else slice(None)
        q = mq[:, hd, sl]; k = mk[:, hd, sl]; v = mv[:, hd, sl]
        gi = gates[(2 * d) * 4 + hd, sl]; gf = gates[(2 * d + 1) * 4 + hd, sl]
        gb = np.zeros((128, 2), np.float32)
        gb[:, 0] = bg[(2 * d) * 4 + hd]; gb[:, 1] = bg[(2 * d + 1) * 4 + hd]
        rqh = rq[:, hd, sl]; rkh = rk[:, hd, sl]; rvh = rv[:, hd, sl]
        gam = _GAMMA[hd] if d == 0 else _GAMMA[3 - hd]
        maps.append(dict(
            m_qT=np.ascontiguousarray(q), m_kT=np.ascontiguousarray(k), m_kM=_tm_blocks(k), m_v=_tm_blocks(v),
            m_gi=np.ascontiguousarray(gi.reshape(-1, 128).T), m_gf=np.ascontiguousarray(gf.reshape(-1, 128).T), m_gb=gb,
            r_qT=np.ascontiguousarray(rqh), r_kT=np.ascontiguousarray(rkh), r_kM=_tm_blocks(rkh), r_v=_tm_blocks(rvh),
            r_tab=_ret_table(gam), U=U))
    return maps


def l2b_inputs(r1):
    qn = _cat_tokens(r1, "o_qn"); qr = _cat_tokens(r1, "o_qr"); kn = _cat_tokens(r1, "o_kn"); v = _cat_tokens(r1, "o_v")
    kr = _cat_tokens(r1, "o_kr")
    return [dict(qn=np.ascontiguousarray(qn[:, h]), qr=np.ascontiguousarray(qr[:, h]), kn=np.ascontiguousarray(kn[:, h]),
                 kr=np.ascontiguousarray(kr), vt=_tm_blocks(v[:, h])) for h in range(NCORE)]


def unit_h_full(r2a, name):
    out = []
    for j in range(NCORE):
        h = np.asarray(r2a[j][name]).transpose(1, 0, 2).reshape(S_FULL, 128)
        out.append(h[::-1] if j // 4 == 1 else h)
    return np.stack(out).reshape(2, 4, S_FULL, 128)


def build_L3():
    P = Prog()
    nc = P.nc
    din, dout, sb = _io(nc)
    T = TPC
    xT_d = din("xT", [128, KC, T])
    hmf_d = din("hmf", [128, 8, 512]); hmb_d = din("hmb", [128, 8, 512])
    hrf_d = din("hrf", [128, 8, 512]); hrb_d = din("hrb", [128, 8, 512])
    go_d = din("go", [128, 8, 512]); gg_d = din("gg", [128, 8, 512])
    gml_d = din("gml", [1, 512]); gret_d = din("gret", [1, 512])
    ymla_d = din("ymla", [128, 8, T], BF16)
    wout_d = din("w_out", [128, KC, DM])
    gffn_d = din("gffn", [128, KC]); gfin_d = din("gfin", [128, KC])
    w1_d = din("w1", [128, KC, 4 * DM])
    w2_d = din("w2", [128, 64, DM])
    ident_d = din("ident", [128, 128], BF16)
    o_xT = dout("o_xT", [128, KC, T])
    o_fin = dout("o_fin", [128, KC, T])

    xT = sb("xTs", [128, KC, T])
    yT = sb("yT", [128, KC, T], BF16)
    ones = sb("ones", [128, 128])
    epst = sb("epst", [128, 1])
    ident = sb("ident_s", [128, 128], BF16)
    gffn = sb("gffn_s", [128, KC]); gfin = sb("gfin_s", [128, KC])
    gbc = sb("gbc", [128, 2, 512])
    rstd = sb("rstd", [128, T]); ntmp = sb("ntmp", [128, T])
    sq_ring = Ring(nc, "sq", 2, [128, T], F32)
    hf_ring = Ring(nc, "hf", 2, [128, 512], F32)
    hb_ring = Ring(nc, "hb", 2, [128, 512], F32)
    gt_ring = Ring(nc, "gt", 2, [128, 512], F32)
    ss_ring = Ring(nc, "ss", 2, [128, 8], F32)
    yb_ring = Ring(nc, "yb", 2, [128, 512], BF16)
    w_ring = Ring(nc, "wr", 3, [128, KC, 256], BF16)
    w2_ring = Ring(nc, "wz", 2, [128, 8, 256], BF16)
    zT_ring = Ring(nc, "zT", 2, [128, 8, T], BF16)
    rl_ring = Ring(nc, "rl", 2, [128, 512], F32)
    of_ring = Ring(nc, "of", 2, [128, T], F32)
    banks = [nc.alloc_psum_tensor(f"pb{i}", [128, 512], F32).ap() for i in range(6)]
    psets = [[(banks[0], "pb0"), (banks[1], "pb1")], [(banks[2], "pb2"), (banks[3], "pb3")], [(banks[4], "pb4"), (banks[5], "pb5")]]
    ptr = Ring(nc, "ptr", 2, [128, 128], BF16, psum=True)
    pctr = [0]

    def next_pset():
        pctr[0] += 1
        return psets[pctr[0] % 3]

    tt = [(0, 512), (512, 1024)]
    for kc in range(KC):
        P.dma("sp", xT[:, kc, :], xT_d[:, kc, :], writes=[f"xT{kc}"], group=f"ldx{kc % 4}")
    for h in range(8):
        P.dma("sp", yT[:, 8 + h, :], ymla_d[:, h, :], writes=[f"yT{8 + h}"], group=f"ldy{h % 4}")
    for t, dsrc, key in ((gffn, gffn_d, "gffn"), (gfin, gfin_d, "gfin"), (ident, ident_d, "ident")):
        P.dma("sp", t, dsrc, writes=[key], group="ldc_" + key)
    P.dma("sp", gbc[:, 0, :], gml_d.partition_broadcast(128), writes=["gbc"], group="ldc_g0")
    P.dma("sp", gbc[:, 1, :], gret_d.partition_broadcast(128), writes=["gbc"], group="ldc_g1")
    P.op("dve", lambda e: e.memset(ones, 1.0), writes=["ones"])
    P.op("dve", lambda e: e.memset(epst, EPS), writes=["epst"])

    for gi_, (hf_d, hb_d, gate_d) in enumerate(((hmf_d, hmb_d, go_d), (hrf_d, hrb_d, gg_d))):
        for b in range(8):
            hf, hfk = hf_ring.next(); hb, hbk = hb_ring.next(); gt, gtk = gt_ring.next()
            P.dma("sp", hf, hf_d[:, b, :], writes=[hfk], group="ld_" + hfk)
            P.dma("sp", hb, hb_d[:, b, :], writes=[hbk], group="ld_" + hbk)
            P.dma("sp", gt, gate_d[:, b, :], writes=[gtk], group="ld_" + gtk)
            P.op("dve", lambda e, hf=hf, hb=hb: e.tensor_tensor(out=hf, in0=hf, in1=hb, op=ALU.add), reads=[hfk, hbk], writes=[hfk])
            P.op("dve", lambda e, gt=gt, gi_=gi_: e.tensor_tensor(out=gt, in0=gt, in1=gbc[:, gi_, :], op=ALU.mult), reads=[gtk, "gbc"], writes=[gtk])
            ss, ssk = ss_ring.next()
            P.op("dve", lambda e, ss=ss: e.memset(ss, 0.0), writes=[ssk])
            for h in range(4):
                P.op("act", lambda e, hf=hf, hb=hb, ss=ss, h=h: e.activation(out=hb[:, h * 128:(h + 1) * 128], in_=hf[:, h * 128:(h + 1) * 128],
                                                                           func=AF.Square, accum_out=ss[:, h:h + 1]),
                     reads=[hfk, ssk], writes=[hbk, ssk])
            P.op("act", lambda e, ss=ss: e.activation(out=ss[:, 4:8], in_=ss[:, 0:4], func=AF.Sqrt, bias=epst[:, 0:1], scale=1.0 / 128),
                 reads=[ssk, "epst"], writes=[ssk])
            P.op("dve", lambda e, ss=ss: e.reciprocal(out=ss[:, 0:4], in_=ss[:, 4:8]), reads=[ssk], writes=[ssk])
            yb, ybk = yb_ring.next()
            for h in range(4):
                P.op("dve", lambda e, hf=hf, gt=gt, ss=ss, yb=yb, h=h: e.scalar_tensor_tensor(
                    out=yb[:, h * 128:(h + 1) * 128], in0=hf[:, h * 128:(h + 1) * 128], scalar=ss[:, h:h + 1], in1=gt[:, h * 128:(h + 1) * 128],
                    op0=ALU.mult, op1=ALU.mult), reads=[hfk, gtk, ssk], writes=[ybk])
            for h in range(4):
                pt, ptk = ptr.next()
                ch = gi_ * 4 + h
                P.op("pe", lambda e, pt=pt, yb=yb, h=h: e.transpose(out=pt, in_=yb[:, h * 128:(h + 1) * 128], identity=ident),
                     reads=[ybk, "ident"], writes=[ptk])
                P.op("act", lambda e, pt=pt, ch=ch, b=b: e.copy(out=yT[:, ch, b * 128:(b + 1) * 128], in_=pt), reads=[ptk], writes=[f"yT{ch}"])

    def load_w(ring, src):
        wt, wk = ring.next()
        P.dma("pool", wt[:, 0:src.shape[1], 0:src.shape[2]], src, writes=[wk], group="ld_" + wk)
        return wt, wk

    yT_keys = [f"yT{c}" for c in range(KC)]

    for mp in range(8):
        wt, wk = load_w(w_ring, wout_d[:, :, mp * 256:(mp + 1) * 256])
        for mm in range(2):
            m = mp * 2 + mm
            ps = next_pset()
            for ti, (a, b) in enumerate(tt):
                for kc in range(KC):
                    P.op("pe", lambda e, ti=ti, a=a, b=b, kc=kc, ps=ps, wt=wt, mm=mm: e.matmul(
                        ps[ti][0], lhsT=wt[:, kc, mm * 128:(mm + 1) * 128], rhs=yT[:, kc, a:b], start=(kc == 0), stop=(kc == KC - 1)),
                        reads=[wk, yT_keys[kc]], writes=[ps[ti][1]])
                P.op("dve", lambda e, ti=ti, a=a, b=b, m=m, ps=ps: e.tensor_tensor(out=xT[:, m, a:b], in0=xT[:, m, a:b], in1=ps[ti][0], op=ALU.add),
                     reads=[f"xT{m}", ps[ti][1]], writes=[f"xT{m}"])

    ps3 = [(banks[0], "pb0"), (banks[1], "pb1")]
    fm_rmsnorm(P, lambda c: (xT[:, c, :], f"xT{c}"), KC, T, [gffn[:, c:c + 1] for c in range(KC)], "gffn",
               [yT[:, c, :] for c in range(KC)], yT_keys, ones, ps3, sq_ring, rstd, ntmp, epst, 1.0 / DM)

    for g in range(8):
        zT, zk = zT_ring.next()
        for fp in range(4):
            f0 = (g * 8 + fp * 2) * 128
            wt, wk = load_w(w_ring, w1_d[:, :, f0:f0 + 256])
            for ff in range(2):
                fl = fp * 2 + ff
                ps = next_pset()
                for ti, (a, b) in enumerate(tt):
                    for kc in range(KC):
                        P.op("pe", lambda e, ti=ti, a=a, b=b, kc=kc, ps=ps, wt=wt, ff=ff: e.matmul(
                            ps[ti][0], lhsT=wt[:, kc, ff * 128:(ff + 1) * 128], rhs=yT[:, kc, a:b], start=(kc == 0), stop=(kc == KC - 1)),
                            reads=[wk, yT_keys[kc]], writes=[ps[ti][1]])
                    rl, rlk = rl_ring.next()
                    P.op("act", lambda e, ti=ti, ps=ps, rl=rl: e.activation(out=rl, in_=ps[ti][0], func=AF.Relu), reads=[ps[ti][1]], writes=[rlk])
                    P.op("dve", lambda e, rl=rl, zT=zT, fl=fl, a=a, b=b: e.tensor_tensor(out=zT[:, fl, a:b], in0=rl, in1=rl, op=ALU.mult),
                         reads=[rlk], writes=[zk])
        for mp in range(8):
            wt, wk = load_w(w2_ring, w2_d[:, g * 8:(g + 1) * 8, mp * 256:(mp + 1) * 256])
            for mm in range(2):
                m = mp * 2 + mm
                ps = next_pset()
                for ti, (a, b) in enumerate(tt):
                    for kc in range(8):
                        P.op("pe", lambda e, ti=ti, a=a, b=b, kc=kc, ps=ps, wt=wt, mm=mm, zT=zT: e.matmul(
                            ps[ti][0], lhsT=wt[:, kc, mm * 128:(mm + 1) * 128], rhs=zT[:, kc, a:b], start=(kc == 0), stop=(kc == 7)),
                            reads=[wk, zk], writes=[ps[ti][1]])
                    P.op("dve", lambda e, ti=ti, a=a, b=b, m=m, ps=ps: e.tensor_tensor(out=xT[:, m, a:b], in0=xT[:, m, a:b], in1=ps[ti][0], op=ALU.add),
                         reads=[f"xT{m}", ps[ti][1]], writes=[f"xT{m}"])

    for kc in range(KC):
        P.dma("sp", o_xT[:, kc, :], xT[:, kc, :], reads=[f"xT{kc}"], group=f"st_x{kc % 4}", is_output=True)

    def fin_out(c):
        return of_ring.next()

    def fin_post(c, oc, ock):
        P.dma("sp", o_fin[:, c, :], oc, reads=[ock], group="st_" + ock, is_output=True)

    fm_rmsnorm(P, lambda c: (xT[:, c, :], f"xT{c}"), KC, T, [gfin[:, c:c + 1] for c in range(KC)], "gfin",
               None, None, ones, ps3, sq_ring, rstd, ntmp, epst, 1.0 / DM, out_fn=fin_out, post=fin_post)
    return P.emit()


def l3_inputs(x_tm, r1, hm, hr, ymla, l, W):
    go = _cat_tokens(r1, "o_mo"); gg = _cat_tokens(r1, "o_rg")
    shared = dict(
        gml=np.ascontiguousarray(W["g_ml_out"][l][None]), gret=np.ascontiguousarray(W["g_ret_out"][l][None]),
        w_out=_kcp(W["w_out"][l]), gffn=np.ascontiguousarray(W["g_ffn"][l].reshape(KC, 128).T),
        gfin=np.ascontiguousarray(W["g_final"].reshape(KC, 128).T), w1=_kcp(W["w_ff1"][l]), w2=_kcp(W["w_ff2"][l]),
        ident=np.eye(128, dtype=np.float32).astype(ml_dtypes.bfloat16))

    def tok_blocks(a_sf, c):
        seg = a_sf[c * TPC:(c + 1) * TPC]
        return np.ascontiguousarray(seg.reshape(8, 128, -1).transpose(1, 0, 2))

    hmf = hm[0].transpose(1, 0, 2).reshape(S_FULL, 512); hmb = hm[1].transpose(1, 0, 2).reshape(S_FULL, 512)
    hrf = hr[0].transpose(1, 0, 2).reshape(S_FULL, 512); hrb = hr[1].transpose(1, 0, 2).reshape(S_FULL, 512)
    go_tm = go.transpose(2, 1, 0).reshape(S_FULL, 512); gg_tm = gg.transpose(2, 1, 0).reshape(S_FULL, 512)
    maps = []
    for c in range(NCORE):
        m = dict(shared)
        seg = x_tm[c * TPC:(c + 1) * TPC]
        m["xT"] = np.ascontiguousarray(seg.T.reshape(KC, 128, TPC).transpose(1, 0, 2))
        m["hmf"] = tok_blocks(hmf, c); m["hmb"] = tok_blocks(hmb, c); m["hrf"] = tok_blocks(hrf, c); m["hrb"] = tok_blocks(hrb, c)
        m["go"] = tok_blocks(go_tm, c); m["gg"] = tok_blocks(gg_tm, c)
        m["ymla"] = np.ascontiguousarray(ymla[:, :, c * TPC:(c + 1) * TPC].transpose(1, 0, 2))
        maps.append(m)
    return maps


def _fm_to_tm(results, name):
    return np.concatenate([np.asarray(r[name]).transpose(2, 1, 0).reshape(TPC, DM) for r in results], 0)


_PROGS = {}


def _prog(name, builder):
    if name not in _PROGS:
        _PROGS[name] = builder()
    return _PROGS[name]


def kernel(**inputs):
    W = {k: np.asarray(v) for k, v in inputs.items()}
    x = np.ascontiguousarray(W["x"][0]).astype(np.float32)
    pos = W["positions"]
    cores = list(range(NCORE))
    fin = None
    for l in range(4):
        r1 = run_bass_kernel_spmd(build_L1(), l1_inputs(x, pos, l, W), core_ids=cores).results
        r2a = run_bass_kernel_spmd(build_L2a(), l2a_inputs(r1, l, W), core_ids=cores).results
        r2b = run_bass_kernel_spmd(build_L2b(), l2b_inputs(r1), core_ids=cores).results
        hm = unit_h_full(r2a, "h_ml"); hr = unit_h_full(r2a, "h_ret")
        ymla = np.stack([np.asarray(r2b[h]["yT"]) for h in range(NCORE)])
        r3 = run_bass_kernel_spmd(build_L3(), l3_inputs(x, r1, hm, hr, ymla, l, W), core_ids=cores).results
        x = _fm_to_tm(r3, "o_xT")
        fin = r3
    out = _fm_to_tm(fin, "o_fin")
    return out[None].astype(np.float32)
```
